# Optimizing a Trainium2 kernel written in Bass

```python
import math
import jax, jax.numpy as jnp
from jax import lax
import numpy as np

D_MODEL = 2048
BATCH = 4
SEQ = 2048
DEPTH = 2

N_META = 16
GRID_W = 64
EPS = 1e-5
N_EVEN = (DEPTH + 1) // 2
N_ODD = DEPTH // 2

SSD_HEAD_DIM = 64
SSD_WIDTH = D_MODEL
SSD_HEADS = SSD_WIDTH // SSD_HEAD_DIM
SSD_GROUPS = 8
SSD_STATE = 128
SSD_GN = SSD_GROUPS * SSD_STATE
SSD_CONV = 5
SSD_CONV_CH = SSD_WIDTH + 2 * SSD_GN
CHUNK = 128

CONV_WIDTH = D_MODEL
CONV_KERNEL = 31

IN_E_SIZES = (SSD_WIDTH,
              SSD_WIDTH,
              SSD_GN,
              SSD_GN,
              SSD_HEADS,
              SSD_HEADS,
              CONV_WIDTH,
              CONV_WIDTH,
              CONV_WIDTH)
IN_E = sum(IN_E_SIZES)
OUT_E = SSD_WIDTH + CONV_WIDTH

NA_HEAD_DIM = 64
NA_WIDTH = D_MODEL
NA_HEADS = NA_WIDTH // NA_HEAD_DIM
NA_KH = 8
NA_KW = 16
IN_O = 4 * NA_WIDTH

kernel_name = "hybrid_ssd_conformer_natten_encoder"


def split_cols(a, sizes):
    idx = [int(i) for i in np.cumsum(sizes)[:-1]]
    return jnp.split(a, idx, axis=-1)


def rmsnorm(x, g):
    xf = x.astype(jnp.float32)
    y = xf * lax.rsqrt(jnp.mean(xf * xf, axis=-1, keepdims=True) + EPS)
    return (y * g.astype(jnp.float32)).astype(x.dtype)


def layernorm(x, g, b):
    xf = x.astype(jnp.float32)
    mu = jnp.mean(xf, axis=-1, keepdims=True)
    var = jnp.mean(jnp.square(xf - mu), axis=-1, keepdims=True)
    y = (xf - mu) * lax.rsqrt(var + EPS)
    return (y * g.astype(jnp.float32) + b.astype(jnp.float32)).astype(x.dtype)


def depthwise_conv(x, w, b):
    y = lax.conv_general_dilated(
        x, w[:, None, :].astype(x.dtype), window_strides=(1,), padding="SAME",
        dimension_numbers=("NWC", "WIO", "NWC"), feature_group_count=x.shape[-1])
    return y + b.astype(x.dtype)


def ssd_scan(xdt, log_a, Bm, Cm):
    b, T, h, p = xdt.shape
    g, n = Bm.shape[2], Bm.shape[3]
    r = h // g
    c = T // CHUNK
    X = xdt.reshape(b, c, CHUNK, g, r, p)
    A = log_a.reshape(b, c, CHUNK, g, r).astype(jnp.float32)
    Bc = Bm.reshape(b, c, CHUNK, g, n)
    Cc = Cm.reshape(b, c, CHUNK, g, n)
    A_cs = jnp.cumsum(A, axis=2)
    tri = jnp.tril(jnp.ones((CHUNK, CHUNK), dtype=bool))[None, None, :, :, None, None]
    seg = A_cs[:, :, :, None] - A_cs[:, :, None, :]
    Lmat = jnp.exp(jnp.where(tri, seg, -jnp.inf))
    CB = jnp.einsum("bclgn,bcsgn->bclsg", Cc, Bc)
    Y_diag = jnp.einsum("bclsgr,bcsgrp->bclgrp", CB[..., None] * Lmat, X)
    Xd = X * jnp.exp(A_cs[:, :, -1:] - A_cs)[..., None]
    states = jnp.einsum("bclgn,bclgrp->bcgrpn", Bc, Xd)
    chunk_decay = jnp.exp(A_cs[:, :, -1])

    def step(carry, inp):
        st, dec = inp
        return carry * dec[..., None, None] + st, carry

    init = jnp.zeros((b, g, r, p, n), dtype=states.dtype)
    _, prev = lax.scan(step, init, (jnp.moveaxis(states, 1, 0), jnp.moveaxis(chunk_decay, 1, 0)))
    prev = jnp.moveaxis(prev, 0, 1)
    Y_off = jnp.einsum("bclgn,bcgrpn->bclgrp", Cc, prev) * jnp.exp(A_cs)[..., None]
    return (Y_diag + Y_off).reshape(b, T, h, p)


def ssd_mixer(xs, Bm, Cm, dt_f, dt_b, z, conv_w, conv_b, dt_bias, A_log, D_skip, norm_g):
    b, L, _ = xs.shape
    xbc = jax.nn.silu(depthwise_conv(jnp.concatenate([xs, Bm, Cm], axis=-1), conv_w, conv_b))
    xs, Bm, Cm = split_cols(xbc, (SSD_WIDTH, SSD_GN, SSD_GN))
    xh = xs.reshape(b, L, SSD_HEADS, SSD_HEAD_DIM)
    Bg = Bm.reshape(b, L, SSD_GROUPS, SSD_STATE)
    Cg = Cm.reshape(b, L, SSD_GROUPS, SSD_STATE)
    pad = (-L) % CHUNK

    def padl(a):
        return jnp.pad(a, [(0, 0), (pad, 0)] + [(0, 0)] * (a.ndim - 2))

    Bp, Cp = padl(Bg), padl(Cg)

    def direction(dt_raw, bias, a_log, reverse):
        dt = jax.nn.softplus(dt_raw.astype(jnp.float32) + bias.astype(jnp.float32))
        log_a = dt * (-jnp.exp(a_log.astype(jnp.float32)))
        xdt = xh.astype(jnp.float32) * dt[..., None]
        args = [padl(xdt), padl(log_a), Bp, Cp]
        if reverse:
            args = [jnp.flip(a, axis=1) for a in args]
        y = ssd_scan(*args)
        if reverse:
            y = jnp.flip(y, axis=1)
        return y[:, pad:]

    y = (direction(dt_f, dt_bias[0], A_log[0], False)
         + direction(dt_b, dt_bias[1], A_log[1], True)
         + D_skip.astype(jnp.float32)[:, None] * xh.astype(jnp.float32))
    y = y.reshape(b, L, SSD_WIDTH).astype(xs.dtype)
    return rmsnorm(y * jax.nn.silu(z), norm_g)


def conformer_conv(val, gate, dw_w, dw_b, ln_g, ln_b):
    u = val * jax.nn.sigmoid(gate)
    u = depthwise_conv(u, dw_w, dw_b)
    return jax.nn.silu(layernorm(u, ln_g, ln_b))


def neighbourhood_attention(q, k, v, rpb):
    b, L, h, d = q.shape
    S = L - N_META
    rows = S // GRID_W
    kh = min(NA_KH, rows)
    kw = NA_KW
    scale = d ** -0.5
    qm, km, vm = q[:, :N_META], k[:, :N_META], v[:, :N_META]
    qg = q[:, N_META:].reshape(b, rows, GRID_W, h, d)
    kg = k[:, N_META:].reshape(b, rows, GRID_W, h, d)
    vg = v[:, N_META:].reshape(b, rows, GRID_W, h, d)

    s_mm = jnp.einsum("bqhd,bkhd->bhqk", qm, km).astype(jnp.float32) * scale
    o_meta = jnp.einsum("bhqk,bkhd->bqhd", jax.nn.softmax(s_mm, axis=-1).astype(v.dtype), vm)

    col = np.arange(GRID_W)
    col_start = np.clip(col - kw // 2, 0, GRID_W - kw)
    col_mask = (col[None, :] >= col_start[:, None]) & (col[None, :] < col_start[:, None] + kw)
    col_idx = np.clip(col[None, :] - col[:, None] + NA_KW - 1, 0, 2 * NA_KW - 2)
    bias_cols = rpb[:, :, col_idx].astype(jnp.float32)

    def row_block(r):
        rs = jnp.clip(r - kh // 2, 0, rows - kh)
        k_rows = lax.dynamic_slice_in_dim(kg, rs, kh, axis=1)
        v_rows = lax.dynamic_slice_in_dim(vg, rs, kh, axis=1)
        q_row = lax.dynamic_index_in_dim(qg, r, axis=1, keepdims=False)
        s = jnp.einsum("bqhd,bjkhd->bhqjk", q_row, k_rows).astype(jnp.float32) * scale
        dr = rs + jnp.arange(kh) - r + NA_KH - 1
        bias = jnp.take(bias_cols, dr, axis=1)
        s = s + jnp.transpose(bias, (0, 2, 1, 3))[None]
        s = jnp.where(col_mask[None, None, :, None, :], s, -jnp.inf)
        s = s.reshape(b, h, GRID_W, kh * GRID_W)
        s_m = jnp.einsum("bqhd,bkhd->bhqk", q_row, km).astype(jnp.float32) * scale
        p = jax.nn.softmax(jnp.concatenate([s, s_m], axis=-1), axis=-1).astype(v.dtype)
        o = (jnp.einsum("bhqj,bjhd->bqhd", p[..., :kh * GRID_W], v_rows.reshape(b, kh * GRID_W, h, d))
             + jnp.einsum("bhqm,bmhd->bqhd", p[..., kh * GRID_W:], vm))
        return o

    o_grid = lax.map(row_block, jnp.arange(rows))
    o_grid = jnp.moveaxis(o_grid, 0, 1).reshape(b, S, h, d)
    return jnp.concatenate([o_meta, o_grid], axis=1)


def even_layer(h, norm_g, w_in, conv_w, conv_b, dt_bias, A_log, D_skip, ssd_norm_g,
               dw_w, dw_b, ln_g, ln_b, w_out):
    u = rmsnorm(h, norm_g)
    proj = jnp.einsum("bld,de->ble", u, w_in)
    z_a, xs, Bm, Cm, dt_f, dt_b, z_b, glu_v, glu_g = split_cols(proj, IN_E_SIZES)
    y_a = ssd_mixer(xs, Bm, Cm, dt_f, dt_b, z_a, conv_w, conv_b, dt_bias, A_log, D_skip, ssd_norm_g)
    y_b = conformer_conv(glu_v, glu_g, dw_w, dw_b, ln_g, ln_b) * jax.nn.silu(z_b)
    return h + jnp.einsum("ble,ed->bld", jnp.concatenate([y_a, y_b], axis=-1), w_out)


def odd_layer(h, norm_g, w_in, rpb, w_out):
    b, L, _ = h.shape
    u = rmsnorm(h, norm_g)
    proj = jnp.einsum("bld,de->ble", u, w_in)
    q, k, v, z = split_cols(proj, (NA_WIDTH,) * 4)
    shp = (b, L, NA_HEADS, NA_HEAD_DIM)
    o = neighbourhood_attention(q.reshape(shp), k.reshape(shp), v.reshape(shp), rpb)
    o = o.reshape(b, L, NA_WIDTH) * jax.nn.silu(z)
    return h + jnp.einsum("ble,ed->bld", o, w_out)


def setup_inputs(seed: int = 0) -> dict:
    key = jax.random.key(seed)
    ks = jax.random.split(key, 24)
    f32 = jnp.float32
    ne, no = N_EVEN, N_ODD

    def nrm(k, shape, scale):
        return jax.random.normal(k, shape, f32) * scale

    dt0 = jnp.exp(jax.random.uniform(ks[6], (ne, 2, SSD_HEADS), f32, math.log(1e-3), math.log(1e-1)))
    return {
        "x": nrm(ks[0], (BATCH, SEQ, D_MODEL), 1.0),
        "meta_tokens": nrm(ks[1], (N_META, D_MODEL), 1.0),
        "e_norm_g": 1.0 + nrm(ks[2], (ne, D_MODEL), 0.02),
        "e_w_in": nrm(ks[3], (ne, D_MODEL, IN_E), D_MODEL ** -0.5),
        "e_conv_w": nrm(ks[4], (ne, SSD_CONV, SSD_CONV_CH), SSD_CONV ** -0.5),
        "e_conv_b": nrm(ks[5], (ne, SSD_CONV_CH), 0.02),
        "e_dt_bias": dt0 + jnp.log(-jnp.expm1(-dt0)),
        "e_A_log": jnp.log(jax.random.uniform(ks[7], (ne, 2, SSD_HEADS), f32, 1.0, 16.0)),
        "e_D": 1.0 + nrm(ks[8], (ne, SSD_HEADS), 0.1),
        "e_ssd_norm_g": 1.0 + nrm(ks[9], (ne, SSD_WIDTH), 0.02),
        "e_dw_w": nrm(ks[10], (ne, CONV_KERNEL, CONV_WIDTH), CONV_KERNEL ** -0.5),
        "e_dw_b": nrm(ks[11], (ne, CONV_WIDTH), 0.02),
        "e_ln_g": 1.0 + nrm(ks[12], (ne, CONV_WIDTH), 0.02),
        "e_ln_b": nrm(ks[13], (ne, CONV_WIDTH), 0.02),
        "e_w_out": nrm(ks[14], (ne, OUT_E, D_MODEL), OUT_E ** -0.5),
        "o_norm_g": 1.0 + nrm(ks[15], (no, D_MODEL), 0.02),
        "o_w_in": nrm(ks[16], (no, D_MODEL, IN_O), D_MODEL ** -0.5),
        "o_rpb": nrm(ks[17], (no, NA_HEADS, 2 * NA_KH - 1, 2 * NA_KW - 1), 0.1),
        "o_w_out": nrm(ks[18], (no, NA_WIDTH, D_MODEL), NA_WIDTH ** -0.5),
        "final_norm_g": 1.0 + nrm(ks[19], (D_MODEL,), 0.02),
    }


def reference(x, meta_tokens, e_norm_g, e_w_in, e_conv_w, e_conv_b, e_dt_bias, e_A_log, e_D,
              e_ssd_norm_g, e_dw_w, e_dw_b, e_ln_g, e_ln_b, e_w_out,
              o_norm_g, o_w_in, o_rpb, o_w_out, final_norm_g):
    b = x.shape[0]
    meta = jnp.broadcast_to(meta_tokens.astype(x.dtype)[None], (b, N_META, x.shape[-1]))
    h = jnp.concatenate([meta, x], axis=1)
    for i in range(DEPTH):
        j = i // 2
        if i % 2 == 0:
            h = even_layer(h, e_norm_g[j], e_w_in[j], e_conv_w[j], e_conv_b[j], e_dt_bias[j],
                           e_A_log[j], e_D[j], e_ssd_norm_g[j], e_dw_w[j], e_dw_b[j],
                           e_ln_g[j], e_ln_b[j], e_w_out[j])
        else:
            h = odd_layer(h, o_norm_g[j], o_w_in[j], o_rpb[j], o_w_out[j])
    h = rmsnorm(h, final_norm_g)
    return h[:, N_META:]
```

```python
import numpy as np
import concourse.bass as bass
import concourse.mybir as mybir
from concourse.bass_utils import run_bass_kernel_spmd

F32 = mybir.dt.float32
BF16 = mybir.dt.bfloat16
AF = mybir.ActivationFunctionType
ALU = mybir.AluOpType

N_CORES = 8


class Buf:
    __slots__ = ("name", "last_write", "readers")

    def __init__(self, name=""):
        self.name = name
        self.last_write = None
        self.readers = []


class Op:
    __slots__ = ("eng", "fn", "deps", "dma", "sig", "sem", "val", "idx", "extra_waits")

    def __init__(self, eng, fn, dma):
        self.eng = eng
        self.fn = fn
        self.dma = dma
        self.deps = set()
        self.sig = False
        self.sem = None
        self.val = 0
        self.extra_waits = []


ENGINES = ("sync", "scalar", "vector", "gpsimd", "tensor")


class Sched:
    def __init__(self, nc, n_dma_sems=20):
        self.nc = nc
        self.ops = []
        self.n_dma_sems = n_dma_sems

    def add(self, eng, fn, reads=(), writes=(), dma=False):
        op = Op(eng, fn, dma)
        op.idx = len(self.ops)
        for b in reads:
            if b.last_write is not None:
                op.deps.add(b.last_write)
        for b in writes:
            if b.last_write is not None:
                op.deps.add(b.last_write)
            for r in b.readers:
                op.deps.add(r)
        for b in reads:
            b.readers.append(op.idx)
        for b in writes:
            b.last_write = op.idx
            b.readers = []
        op.deps.discard(op.idx)
        self.ops.append(op)
        return op

    def emit(self, stack):
        nc = self.nc
        ops = self.ops
        for op in ops:
            keep = set()
            for d in op.deps:
                p = ops[d]
                if (not p.dma) and (not op.dma) and p.eng == op.eng:
                    if p.eng == "tensor":
                        continue
                keep.add(d)
                p.sig = True
            op.deps = keep
        for op in ops:
            if op.dma:
                op.sig = True
        prog = {e: stack.enter_context(nc.semaphore("prog_" + e)) for e in ENGINES}
        dma_sems = {}
        for e in ("sync", "gpsimd", "scalar"):
            dma_sems[e] = [stack.enter_context(nc.semaphore("dma_%s_%d" % (e, i)))
                           for i in range(self.n_dma_sems)]
        dma_cnt = {e: [0] * self.n_dma_sems for e in dma_sems}
        dma_rr = {e: 0 for e in dma_sems}
        cnt = {e: 0 for e in ENGINES}
        for op in ops:
            if op.dma:
                i = dma_rr[op.eng]
                dma_rr[op.eng] = (i + 1) % self.n_dma_sems
                s = dma_sems[op.eng][i]
                if dma_cnt[op.eng][i] > 0:
                    op.extra_waits.append((s, dma_cnt[op.eng][i]))
                dma_cnt[op.eng][i] += 16
                op.sem, op.val = s, dma_cnt[op.eng][i]
            elif op.sig:
                cnt[op.eng] += 1
                op.sem, op.val = prog[op.eng], cnt[op.eng]
        final_waits = []
        for e in dma_sems:
            for i in range(self.n_dma_sems):
                if dma_cnt[e][i] > 0:
                    final_waits.append((dma_sems[e][i], dma_cnt[e][i]))
        by_eng = {e: [op for op in ops if op.eng == e] for e in ENGINES}
        block = stack.enter_context(nc.Block())

        def run(engname, eng):
            known = {}
            for op in by_eng[engname]:
                waits = {}
                for d in op.deps:
                    p = ops[d]
                    k = id(p.sem)
                    if known.get(k, 0) >= p.val:
                        continue
                    if k not in waits or waits[k][1] < p.val:
                        waits[k] = (p.sem, p.val)
                for (s, v) in op.extra_waits:
                    k = id(s)
                    if known.get(k, 0) >= v:
                        continue
                    if k not in waits or waits[k][1] < v:
                        waits[k] = (s, v)
                for k, (s, v) in waits.items():
                    eng.wait_ge(s, v)
                    known[k] = v
                inst = op.fn(eng)
                if op.sig:
                    inst.then_inc(op.sem, 16 if op.dma else 1)
            if engname == "sync":
                for (s, v) in final_waits:
                    eng.wait_ge(s, v)

        @block.sync
        def _(e):
            run("sync", e)

        @block.scalar
        def _(e):
            run("scalar", e)

        @block.vector
        def _(e):
            run("vector", e)

        @block.gpsimd
        def _(e):
            run("gpsimd", e)

        @block.tensor
        def _(e):
            run("tensor", e)


class Ctx:
    def __init__(self, stack):
        self.nc = bass.Bass("TRN2", target_bir_lowering=False)
        self.stack = stack
        self.s = Sched(self.nc)
        self._n = 0

    def sb(self, shape, dt, name=None):
        self._n += 1
        return self.stack.enter_context(
            self.nc.sbuf_tensor(name or ("sb%d" % self._n), list(shape), dt))

    def ps(self, shape, dt=F32, name=None):
        self._n += 1
        return self.stack.enter_context(
            self.nc.psum_tensor(name or ("ps%d" % self._n), list(shape), dt))

    def dram_in(self, name, shape, dt=F32):
        return self.nc.dram_tensor(name, list(shape), dt, kind="ExternalInput").ap()

    def dram_out(self, name, shape, dt=F32):
        return self.nc.dram_tensor(name, list(shape), dt, kind="ExternalOutput").ap()

    def finish(self):
        self.s.emit(self.stack)
        return self.nc


D = 2048
NB_ = 4
SEQ = 2048
NMETA = 16
L = SEQ + NMETA
T = L // 2
TB = 344
NTB = 3
KC = D // 128
EPS = 1e-5
IN_E = 12352
IN_E_PAD = 97 * 128


def _cp(eng_name):
    return eng_name


class Gemm:
    def __init__(self, stack):
        self.c = Ctx(stack)
        c = self.c
        self.nc = c.nc
        self.s = c.s
        self.banks = [c.ps([128, 512], F32, name="bank%d" % i) for i in range(8)]
        self.bankbuf = [Buf("bank%d" % i) for i in range(8)]
        self.ones = c.sb([128, 128], BF16, name="ones")
        self.ones_b = Buf("ones")
        ones = self.ones
        self.s.add("vector", lambda e: e.memset(ones[:], 1.0), writes=[self.ones_b])
        self.flip = 0

    def load_chunks(self, dram, xs, xb, nchunks, row0=0, eng="sync"):
        for k in range(nchunks):
            self.s.add(eng, lambda e, k=k: e.dma_start(
                out=xs[:, k, :], in_=dram[row0 + k * 128: row0 + (k + 1) * 128, :]),
                writes=[xb[k]], dma=True)

    def load_vec(self, name, ncols):
        c = self.c
        d = c.dram_in(name, [128, ncols])
        t = c.sb([128, ncols], F32, name=name + "_sb")
        b = Buf(name)
        self.s.add("sync", lambda e: e.dma_start(out=t[:], in_=d[:, :]), writes=[b], dma=True)
        return t, b

    def stats(self, xs, xb, nchunks, sq, sqb, bank_sets, want_mean=False, xbf=None, xbfb=None):
        s = self.s
        ones, ones_b = self.ones, self.ones_b
        for k in range(nchunks):
            j = k % 2
            s.add("scalar", lambda e, k=k, j=j: e.activation(out=sq[:, j, :], in_=xs[:, k, :], func=AF.Square),
                  reads=[xb[k]], writes=[sqb[j]])
            if want_mean:
                s.add("gpsimd", lambda e, k=k, j=j: e.tensor_copy(out=xbf[:, j, :], in_=xs[:, k, :]),
                      reads=[xb[k]], writes=[xbfb[j]])
            for tb in range(NTB):
                bk = bank_sets[0][tb]
                s.add("tensor", lambda e, k=k, j=j, tb=tb, bk=bk: e.matmul(
                    self.banks[bk][:, :TB], lhsT=ones[:], rhs=sq[:, j, tb * TB:(tb + 1) * TB],
                    start=(k == 0), stop=(k == nchunks - 1)),
                    reads=[ones_b, sqb[j]], writes=[self.bankbuf[bk]])
                if want_mean:
                    bk2 = bank_sets[1][tb]
                    s.add("tensor", lambda e, k=k, j=j, tb=tb, bk2=bk2: e.matmul(
                        self.banks[bk2][:, :TB], lhsT=ones[:], rhs=xbf[:, j, tb * TB:(tb + 1) * TB],
                        start=(k == 0), stop=(k == nchunks - 1)),
                        reads=[ones_b, xbfb[j]], writes=[self.bankbuf[bk2]])

    def rstd_from_ssq(self, bank_set, rstd, rstdb, n):
        s = self.s
        for tb in range(NTB):
            bk = bank_set[tb]
            s.add("scalar", lambda e, tb=tb, bk=bk: e.activation(
                out=rstd[:, tb * TB:(tb + 1) * TB], in_=self.banks[bk][:, :TB], func=AF.Sqrt,
                scale=1.0 / n, bias=EPS), reads=[self.bankbuf[bk]], writes=[rstdb])
        s.add("vector", lambda e: e.reciprocal(out=rstd[:], in_=rstd[:]), reads=[rstdb], writes=[rstdb])

    def rmsnorm_fm(self, xs, xb, g, gb, act, actb, act0, bank_set, sq, sqb, rstd, rstdb):
        s = self.s
        self.stats(xs, xb, KC, sq, sqb, [bank_set])
        self.rstd_from_ssq(bank_set, rstd, rstdb, D)
        for k in range(KC):
            s.add("vector", lambda e, k=k: e.scalar_tensor_tensor(
                out=act[:, act0 + k, :], in0=xs[:, k, :], scalar=g[:, k:k + 1], op0=ALU.mult,
                in1=rstd[:], op1=ALU.mult), reads=[xb[k], gb, rstdb], writes=[actb[act0 + k]])

    def main_loop(self, wl, nblocks, kchunks, act, actb, epilogue):
        c, s = self.c, self.s
        nwb = 3
        wts = [c.sb([128, kchunks * 128], BF16, name="w%d" % i) for i in range(nwb)]
        wtb = [Buf("w%d" % i) for i in range(nwb)]

        def load_w(nb):
            i = nb % nwb
            for k0 in range(0, kchunks, 16):
                s.add("gpsimd", lambda e, nb=nb, i=i, k0=k0: e.dma_start(
                    out=wts[i][:, k0 * 128:(k0 + 16) * 128], in_=wl[nb, :, k0 * 128:(k0 + 16) * 128]),
                    writes=[wtb[i]], dma=True)

        sets = [[0, 1, 2], [3, 4, 5]]
        for nb in range(min(2, nblocks)):
            load_w(nb)
        for nb in range(nblocks):
            if nb + 2 < nblocks:
                load_w(nb + 2)
            i = nb % nwb
            st = sets[nb % 2]
            for tb in range(NTB):
                bk = st[tb]

                def mm(e, i=i, tb=tb, bk=bk):
                    inst = None
                    for k in range(kchunks):
                        inst = e.matmul(self.banks[bk][:, :TB], lhsT=wts[i][:, k * 128:(k + 1) * 128],
                                        rhs=act[:, k, tb * TB:(tb + 1) * TB],
                                        start=(k == 0), stop=(k == kchunks - 1))
                    return inst
                s.add("tensor", mm, reads=[wtb[i]] + list(actb[:kchunks]), writes=[self.bankbuf[bk]])
            epilogue(nb, st)

    def evac_engine(self):
        self.flip ^= 1
        return "scalar" if self.flip else "vector"

    def copy_out(self, eng, out_ap, in_ap):
        if eng == "scalar":
            return lambda e: e.copy(out=out_ap, in_=in_ap)
        return lambda e: e.tensor_copy(out=out_ap, in_=in_ap)


def build_in_proj(nblocks, name_w="w"):
    from contextlib import ExitStack
    with ExitStack() as stack:
        G = Gemm(stack)
        c, s = G.c, G.s
        hT = c.dram_in("hT", [D, T])
        wl = c.dram_in("w", [nblocks, 128, D])
        outT = c.dram_out("outT", [nblocks * 128, T])
        g, gb = G.load_vec("g", KC)
        xs = c.sb([128, KC, T], F32, name="xs")
        xb = [Buf("x%d" % k) for k in range(KC)]
        act = c.sb([128, KC, T], BF16, name="act")
        actb = [Buf("a%d" % k) for k in range(KC)]
        sq = c.sb([128, 2, T], BF16, name="sq")
        sqb = [Buf(), Buf()]
        rstd = c.sb([128, T], F32, name="rstd")
        rstdb = Buf("rstd")
        G.load_chunks(hT, xs, xb, KC)
        G.rmsnorm_fm(xs, xb, g, gb, act, actb, 0, [0, 1, 2], sq, sqb, rstd, rstdb)
        stage = c.sb([128, 2, T], F32, name="stage")
        stageb = [Buf("st0"), Buf("st1")]

        def epi(nb, st):
            j = nb % 2
            for tb in range(NTB):
                eng = G.evac_engine()
                s.add(eng, G.copy_out(eng, stage[:, j, tb * TB:(tb + 1) * TB], G.banks[st[tb]][:, :TB]),
                      reads=[G.bankbuf[st[tb]]], writes=[stageb[j]])
            s.add("sync", lambda e, nb=nb, j=j: e.dma_start(
                out=outT[nb * 128:(nb + 1) * 128, :], in_=stage[:, j, :]), reads=[stageb[j]], dma=True)

        G.main_loop(wl, nblocks, KC, act, actb, epi)
        return c.finish()


LB = 344
NLB = 6


def make_ident(c, s, dt=F32, name="ident"):
    ident = c.sb([128, 128], dt, name=name)
    ib = Buf(name)
    s.add("vector", lambda e: e.memset(ident[:], 1.0), writes=[ib])
    s.add("gpsimd", lambda e: e.affine_select(out=ident[:], in_=ident[:], pattern=[[-1, 128]], base=0,
                                              channel_multiplier=1, compare_op=ALU.is_equal, fill=0.0),
          reads=[ib], writes=[ib])
    return ident, ib


def build_conv(n5, n31):
    from contextlib import ExitStack
    with ExitStack() as stack:
        c = Ctx(stack)
        s = c.s
        xin = c.dram_in("xin", [n5, 128, L])
        cw5 = c.dram_in("cw5", [128, n5, 5])
        cb5 = c.dram_in("cb5", [128, n5])
        gv = c.dram_in("gv", [n31, 128, L])
        gg = c.dram_in("gg", [n31, 128, L])
        cw31 = c.dram_in("cw31", [128, n31, 31])
        cb31 = c.dram_in("cb31", [128, n31])
        xout = c.dram_out("xout", [n5, 128, L])
        cout = c.dram_out("cout", [n31, 128, L])
        banks = [c.ps([128, 512], F32, name="bank%d" % i) for i in range(8)]
        bankb = [Buf() for _ in range(8)]
        ident, ib = make_ident(c, s)
        w5 = c.sb([128, n5, 5], F32, name="w5")
        b5 = c.sb([128, n5], F32, name="b5")
        w31 = c.sb([128, n31, 31], F32, name="w31")
        b31 = c.sb([128, n31], F32, name="b31")
        wb = Buf("wts")
        for (t, d) in ((w5, cw5), (b5, cb5), (w31, cw31), (b31, cb31)):
            s.add("sync", lambda e, t=t, d=d: e.dma_start(out=t[:], in_=d), writes=[wb], dma=True)
        PADM = 15
        xf = [c.sb([128, L], F32, name="xf%d" % i) for i in range(2)]
        xfb = [Buf() for _ in range(2)]
        gf = [c.sb([128, L], F32, name="gf%d" % i) for i in range(2)]
        gfb = [Buf() for _ in range(2)]
        xp = [c.sb([128, L + 2 * PADM], BF16, name="xp%d" % i) for i in range(2)]
        xpb = [Buf() for _ in range(2)]
        dg = [c.sb([128, 31, 128], BF16, name="dg%d" % i) for i in range(2)]
        dgb = [Buf() for _ in range(2)]
        st = [c.sb([128, L], F32, name="st%d" % i) for i in range(2)]
        stb = [Buf() for _ in range(2)]
        for i in range(2):
            s.add("vector", lambda e, i=i: e.memset(xp[i][:], 0.0), writes=[xpb[i]])
        items = [("c5", i) for i in range(n5)] + [("c31", i) for i in range(n31)]
        for it, (kind, i) in enumerate(items):
            j = it % 2
            ntap = 5 if kind == "c5" else 31
            pad = ntap // 2
            if kind == "c5":
                s.add("sync", lambda e, i=i, j=j: e.dma_start(out=xf[j][:], in_=xin[i]), writes=[xfb[j]], dma=True)
                s.add("gpsimd", lambda e, j=j: e.tensor_copy(out=xp[j][:, PADM:PADM + L], in_=xf[j][:]),
                      reads=[xfb[j]], writes=[xpb[j]])
                wt, bt = w5, b5
            else:
                s.add("sync", lambda e, i=i, j=j: e.dma_start(out=xf[j][:], in_=gv[i]), writes=[xfb[j]], dma=True)
                s.add("sync", lambda e, i=i, j=j: e.dma_start(out=gf[j][:], in_=gg[i]), writes=[gfb[j]], dma=True)
                s.add("scalar", lambda e, j=j: e.activation(out=gf[j][:], in_=gf[j][:], func=AF.Sigmoid),
                      reads=[gfb[j]], writes=[gfb[j]])
                s.add("vector", lambda e, j=j: e.tensor_tensor(out=xp[j][:, PADM:PADM + L], in0=xf[j][:],
                                                               in1=gf[j][:], op=ALU.mult),
                      reads=[xfb[j], gfb[j]], writes=[xpb[j]])
                wt, bt = w31, b31
            for tp in range(ntap):
                eng = "vector" if tp % 2 == 0 else "gpsimd"
                s.add(eng, lambda e, j=j, tp=tp, wt=wt, i=i: e.tensor_scalar(
                    out=dg[j][:, tp, :], in0=ident[:], scalar1=wt[:, i, tp:tp + 1], scalar2=None, op0=ALU.mult),
                    reads=[ib, wb], writes=[dgb[j]])
            for tb in range(NLB):
                bk = (it * NLB + tb) % 8

                def mm(e, j=j, tb=tb, bk=bk, ntap=ntap, pad=pad):
                    inst = None
                    for tp in range(ntap):
                        o = PADM - pad + tp + tb * LB
                        inst = e.matmul(banks[bk][:, :LB], lhsT=dg[j][:, tp, :], rhs=xp[j][:, o:o + LB],
                                        start=(tp == 0), stop=(tp == ntap - 1))
                    return inst
                s.add("tensor", mm, reads=[dgb[j], xpb[j]], writes=[bankb[bk]])
                fn = AF.Silu if kind == "c5" else AF.Identity
                s.add("scalar", lambda e, j=j, tb=tb, bk=bk, fn=fn, bt=bt, i=i: e.activation(
                    out=st[j][:, tb * LB:(tb + 1) * LB], in_=banks[bk][:, :LB], func=fn, bias=bt[:, i:i + 1]),
                    reads=[bankb[bk], wb], writes=[stb[j]])
            dst = xout if kind == "c5" else cout
            s.add("sync", lambda e, i=i, j=j, dst=dst: e.dma_start(out=dst[i], in_=st[j][:]), reads=[stb[j]], dma=True)
        return c.finish()


NCH = 17
LP = NCH * 128
HL = 16
GL = 4


def build_ssd():
    from contextlib import ExitStack
    with ExitStack() as stack:
        c = Ctx(stack)
        s = c.s
        nc = c.nc
        xtok = c.dram_in("xtok", [NCH, 128, 1024])
        btok = c.dram_in("btok", [NCH, 128, 512])
        bT_d = c.dram_in("bT", [GL, 128, LP])
        cT_d = c.dram_in("cT", [GL, 128, LP])
        dtraw = c.dram_in("dtraw", [NCH, 128, 32])
        dtb_d = c.dram_in("dtbias", [128, 32])
        alog_d = c.dram_in("alog", [128, 32])
        dcol_d = c.dram_in("dcol", [128, 8])
        xsT = c.dram_in("xsT", [8, 128, LP])
        zaT = c.dram_in("zaT", [8, 128, LP])
        ygT = c.dram_out("ygT", [8, 128, LP])
        banks = [c.ps([128, 512], F32, name="bank%d" % i) for i in range(8)]
        bankb = [Buf("bank%d" % i) for i in range(8)]

        ones = c.sb([128, 128], F32, name="ones")
        tri_f = c.sb([128, 128], F32, name="tri_f")
        tri_b = c.sb([128, 128], F32, name="tri_b")
        cb = Buf("consts")
        s.add("vector", lambda e: e.memset(ones[:], 1.0), writes=[cb])
        s.add("vector", lambda e: e.memset(tri_f[:], 1.0), writes=[cb])
        s.add("vector", lambda e: e.memset(tri_b[:], 1.0), writes=[cb])
        s.add("gpsimd", lambda e: e.affine_select(out=tri_f[:], in_=tri_f[:], pattern=[[1, 128]], base=0,
                                                  channel_multiplier=-1, compare_op=ALU.is_ge, fill=0.0),
              reads=[cb], writes=[cb])
        s.add("gpsimd", lambda e: e.affine_select(out=tri_b[:], in_=tri_b[:], pattern=[[-1, 128]], base=0,
                                                  channel_multiplier=1, compare_op=ALU.is_ge, fill=0.0),
              reads=[cb], writes=[cb])
        tri = [tri_f, tri_b]

        dtb = c.sb([128, 32], F32, name="dtb_sb")
        negA = c.sb([128, 32], F32, name="negA_sb")
        dcol = c.sb([128, 8], F32, name="dcol_sb")
        vb = Buf("vecs")
        s.add("sync", lambda e: e.dma_start(out=dtb[:], in_=dtb_d), writes=[vb], dma=True)
        s.add("sync", lambda e: e.dma_start(out=negA[:], in_=alog_d), writes=[vb], dma=True)
        s.add("sync", lambda e: e.dma_start(out=dcol[:], in_=dcol_d), writes=[vb], dma=True)
        s.add("scalar", lambda e: e.activation(out=negA[:], in_=negA[:], func=AF.Exp), reads=[vb], writes=[vb])
        s.add("vector", lambda e: e.tensor_scalar(out=negA[:], in0=negA[:], scalar1=-1.0, scalar2=None,
                                                  op0=ALU.mult), reads=[vb], writes=[vb])

        SH = [128, NCH, 32]
        dt = c.sb(SH, F32, name="dt")
        G = c.sb(SH, F32, name="G")
        S = c.sb(SH, F32, name="S")
        t1, t2 = G, S
        a = c.sb(SH, F32, name="a")
        db = Buf("dt")
        s.add("sync", lambda e: e.dma_start(out=dt[:], in_=dtraw.rearrange("c p h -> p c h")), writes=[db], dma=True)
        s.add("vector", lambda e: e.tensor_tensor(out=dt[:], in0=dt[:], in1=dtb[:].unsqueeze(1).to_broadcast(SH),
                                                  op=ALU.add), reads=[db, vb], writes=[db])
        s.add("scalar", lambda e: e.activation(out=t1[:], in_=dt[:], func=AF.Abs), reads=[db], writes=[db])
        s.add("scalar", lambda e: e.activation(out=t1[:], in_=t1[:], func=AF.Exp, scale=-1.0), reads=[db], writes=[db])
        s.add("scalar", lambda e: e.activation(out=t1[:], in_=t1[:], func=AF.Ln, bias=1.0), reads=[db], writes=[db])
        s.add("vector", lambda e: e.tensor_scalar(out=t2[:], in0=dt[:], scalar1=0.0, scalar2=None, op0=ALU.max),
              reads=[db], writes=[db])
        s.add("vector", lambda e: e.tensor_tensor(out=dt[:], in0=t1[:], in1=t2[:], op=ALU.add), reads=[db], writes=[db])
        s.add("vector", lambda e: e.memset(dt[0:112, 0, :], 0.0), reads=[db], writes=[db])
        s.add("vector", lambda e: e.tensor_tensor(out=a[:], in0=dt[:], in1=negA[:].unsqueeze(1).to_broadcast(SH),
                                                  op=ALU.mult), reads=[db, vb], writes=[db])
        for d in range(2):
            s.add("tensor", lambda e, d=d: e.matmul(banks[d][:, :NCH * 16].rearrange("p (c h) -> p c h", h=16),
                                                    lhsT=tri[d][:], rhs=a[:, :, d * 16:(d + 1) * 16],
                                                    start=True, stop=True), reads=[cb, db], writes=[bankb[d]])
            s.add("vector", lambda e, d=d: e.tensor_copy(
                out=G[:, :, d * 16:(d + 1) * 16], in_=banks[d][:, :NCH * 16].rearrange("p (c h) -> p c h", h=16)),
                reads=[bankb[d]], writes=[db])
            s.add("tensor", lambda e, d=d: e.matmul(banks[2 + d][:, :NCH * 16].rearrange("p (c h) -> p c h", h=16),
                                                    lhsT=ones[:], rhs=a[:, :, d * 16:(d + 1) * 16],
                                                    start=True, stop=True), reads=[cb, db], writes=[bankb[2 + d]])
            s.add("vector", lambda e, d=d: e.tensor_copy(
                out=S[:, :, d * 16:(d + 1) * 16], in_=banks[2 + d][:, :NCH * 16].rearrange("p (c h) -> p c h", h=16)),
                reads=[bankb[2 + d]], writes=[db])
        negG = c.sb(SH, F32, name="negG")
        scst = c.sb(SH, F32, name="scst")
        edec = c.sb(SH, F32, name="edec")
        s.add("vector", lambda e: e.tensor_scalar(out=negG[:], in0=G[:], scalar1=-1.0, scalar2=None, op0=ALU.mult),
              reads=[db], writes=[db])
        s.add("vector", lambda e: e.tensor_tensor(out=scst[:], in0=S[:], in1=G[:], op=ALU.subtract),
              reads=[db], writes=[db])
        s.add("scalar", lambda e: e.activation(out=scst[:], in_=scst[:], func=AF.Exp), reads=[db], writes=[db])
        s.add("vector", lambda e: e.tensor_tensor(out=scst[:], in0=scst[:], in1=dt[:], op=ALU.mult),
              reads=[db], writes=[db])
        s.add("scalar", lambda e: e.activation(out=edec[:], in_=S[:], func=AF.Exp), reads=[db], writes=[db])

        bT = c.sb([128, GL, LP], BF16, name="bT_sb")
        cT = c.sb([128, GL, LP], BF16, name="cT_sb")
        bcb = Buf("bc")
        for g in range(GL):
            for (dst, src) in ((bT, bT_d), (cT, cT_d)):
                for hf in range(2):
                    s.add("gpsimd", lambda e, g=g, dst=dst, src=src, hf=hf: e.dma_start(
                        out=dst[:, g, hf * 1088:(hf + 1) * 1088], in_=src[g, :, hf * 1088:(hf + 1) * 1088]),
                        writes=[bcb], dma=True)

        prevs = c.sb([128, 2, NCH, 1024], BF16, name="prevs")
        prevb = [[Buf() for _ in range(NCH)] for _ in range(2)]
        P = [c.sb([128, 1024], F32, name="P%d" % d) for d in range(2)]
        Pb = [Buf(), Buf()]
        xc = [c.sb([128, 1024], F32, name="xc%d" % i) for i in range(2)]
        xcb = [Buf(), Buf()]
        xd = [c.sb([128, 1024], BF16, name="xd%d" % i) for i in range(4)]
        xdb = [Buf() for _ in range(4)]
        for d in range(2):
            s.add("vector", lambda e, d=d: e.memset(P[d][:], 0.0), writes=[Pb[d]])
        xdt = [[c.sb([128, 1024], BF16, name="xdt%d_%d" % (i, d)) for d in range(2)] for i in range(2)]
        xdtb = [[Buf(), Buf()] for _ in range(2)]
        cbm = [[c.sb([128, 128], F32, name="cbm%d_%d" % (i, d)) for d in range(2)] for i in range(2)]
        cbmb = [[Buf(), Buf()] for _ in range(2)]
        NR = 2
        rhs_t = [c.sb([128, 4, 128], F32, name="rhs%d" % i) for i in range(NR)]
        rhsb = [Buf() for _ in range(NR)]
        Lt = [c.sb([128, 4, 128], F32, name="Lt%d" % i) for i in range(NR)]
        Ltb = [Buf() for _ in range(NR)]
        Et = [c.sb([128, 4, 128], F32, name="Et%d" % i) for i in range(NR)]
        Etb = [Buf() for _ in range(NR)]
        NM = 4
        Mt = [c.sb([128, 4, 128], BF16, name="Mt%d" % i) for i in range(NM)]
        Mtb = [Buf() for _ in range(NM)]
        Cp = [c.sb([128, 4, 128], BF16, name="Cp%d" % i) for i in range(NM)]
        Cpb = [Buf() for _ in range(NM)]
        bk_cm = nc.sbuf_tensor("btok_sb", [128, NCH, 512], BF16)
        bk_sb = bk_cm.__enter__()
        bkbuf = Buf("btok")
        for ch in range(NCH):
            s.add("gpsimd", lambda e, ch=ch: e.dma_start(out=bk_sb[:, ch, :], in_=btok[ch]), writes=[bkbuf], dma=True)
        it = 0
        for step in range(NCH):
            for d in range(2):
                ch = step if d == 0 else NCH - 1 - step
                j = it % 2
                jj = it % 4
                it += 1
                s.add("sync", lambda e, ch=ch, j=j: e.dma_start(out=xc[j][:], in_=xtok[ch]), writes=[xcb[j]], dma=True)
                s.add("scalar", lambda e, d=d, ch=ch: e.copy(out=prevs[:, d, ch, :], in_=P[d][:]),
                      reads=[Pb[d]], writes=[prevb[d][ch]])
                if step == NCH - 1:
                    continue
                s.add("gpsimd", lambda e, ch=ch, d=d, j=j, jj=jj: e.tensor_tensor(
                    out=xd[jj][:].rearrange("p (h q) -> p h q", q=64),
                    in0=xc[j][:].rearrange("p (h q) -> p h q", q=64),
                    in1=scst[:, ch, d * 16:(d + 1) * 16].unsqueeze(2).to_broadcast([128, 16, 64]), op=ALU.mult),
                    reads=[xcb[j], db], writes=[xdb[jj]])
                bks = [4 + 2 * d, 5 + 2 * d]
                for g in range(GL):
                    bk = bks[g // 2]
                    s.add("tensor", lambda e, ch=ch, g=g, jj=jj, bk=bk: e.matmul(
                        banks[bk][:, (g % 2) * 256:(g % 2 + 1) * 256], lhsT=bk_sb[:, ch, g * 128:(g + 1) * 128],
                        rhs=xd[jj][:, g * 256:(g + 1) * 256], start=True, stop=True),
                        reads=[bkbuf, xdb[jj]], writes=[bankb[bk]])
                s.add("vector", lambda e, d=d, ch=ch: e.tensor_tensor(
                    out=P[d][:].rearrange("p (h q) -> p h q", q=64), in0=P[d][:].rearrange("p (h q) -> p h q", q=64),
                    in1=edec[:, ch, d * 16:(d + 1) * 16].unsqueeze(2).to_broadcast([128, 16, 64]), op=ALU.mult),
                    reads=[db, Pb[d]], writes=[Pb[d]])
                for hf in range(2):
                    s.add("vector", lambda e, d=d, hf=hf, bk=bks[hf]: e.tensor_tensor(
                        out=P[d][:, hf * 512:(hf + 1) * 512], in0=P[d][:, hf * 512:(hf + 1) * 512],
                        in1=banks[bk][:, :], op=ALU.add), reads=[Pb[d], bankb[bk]], writes=[Pb[d]])

        bk_cm.__exit__(None, None, None)
        xs_s = [c.sb([128, 8, 128], F32, name="xs_s%d" % i) for i in range(2)]
        xs_sb = [Buf(), Buf()]
        za_s = [c.sb([128, 8, 128], F32, name="za_s%d" % i) for i in range(2)]
        za_sb = [Buf(), Buf()]
        yo = [c.sb([128, 8, 128], F32, name="yo%d" % i) for i in range(2)]
        yob = [Buf(), Buf()]
        s.add("vector", lambda e: e.memset(yo[0][:, 0, 0:2], 0.0),
              writes=[bkbuf] + xs_sb + za_sb + yob)
        ir = 0
        im = 0
        for ch in range(NCH):
            j = ch % 2
            cs_ = slice(ch * 128, (ch + 1) * 128)
            s.add("sync", lambda e, ch=ch, j=j: e.dma_start(out=xc[j][:], in_=xtok[ch]), writes=[xcb[j]], dma=True)
            s.add("sync", lambda e, j=j, cs_=cs_: e.dma_start(out=xs_s[j][:], in_=xsT[:, :, cs_].rearrange("i p t -> p i t")),
                  writes=[xs_sb[j]], dma=True)
            s.add("sync", lambda e, j=j, cs_=cs_: e.dma_start(out=za_s[j][:], in_=zaT[:, :, cs_].rearrange("i p t -> p i t")),
                  writes=[za_sb[j]], dma=True)
            for d in range(2):
                s.add("gpsimd", lambda e, ch=ch, d=d, j=j: e.tensor_tensor(
                    out=xdt[j][d][:].rearrange("p (h q) -> p h q", q=64),
                    in0=xc[j][:].rearrange("p (h q) -> p h q", q=64),
                    in1=dt[:, ch, d * 16:(d + 1) * 16].unsqueeze(2).to_broadcast([128, 16, 64]), op=ALU.mult),
                    reads=[xcb[j], db], writes=[xdtb[j][d]])
            ybanks = [4 + 2 * j, 5 + 2 * j]
            for g in range(GL):
                gj = (ch * GL + g) % 2
                bkc = gj
                s.add("tensor", lambda e, g=g, cs_=cs_, bkc=bkc: e.matmul(
                    banks[bkc][:, :128], lhsT=bT[:, g, cs_], rhs=cT[:, g, cs_], start=True, stop=True),
                    reads=[bcb], writes=[bankb[bkc]])
                for d in range(2):
                    s.add("vector", lambda e, gj=gj, d=d, bkc=bkc: e.tensor_tensor(
                        out=cbm[gj][d][:], in0=banks[bkc][:, :128], in1=tri[d][:], op=ALU.mult),
                        reads=[bankb[bkc], cb], writes=[cbmb[gj][d]])
                mi = []
                for d in range(2):
                    r = ir % NR
                    ir += 1
                    m = im % NM
                    im += 1
                    mi.append(m)
                    bkb = 2 + (ch * GL * 2 + g * 2 + d) % 2
                    hs = slice(d * 16 + g * 4, d * 16 + g * 4 + 4)
                    s.add("gpsimd", lambda e, r=r, d=d, ch=ch, hs=hs: e.tensor_tensor(
                        out=rhs_t[r][:], in0=tri[d][:].unsqueeze(1).to_broadcast([128, 4, 128]),
                        in1=a[:, ch, hs].unsqueeze(2).to_broadcast([128, 4, 128]), op=ALU.mult),
                        reads=[cb, db], writes=[rhsb[r]])
                    s.add("tensor", lambda e, r=r, bkb=bkb: e.matmul(
                        banks[bkb][:, :], lhsT=ones[:], rhs=rhs_t[r][:].rearrange("p h l -> p (h l)"),
                        start=True, stop=True), reads=[cb, rhsb[r]], writes=[bankb[bkb]])
                    for h in range(4):
                        hh = d * 16 + g * 4 + h
                        s.add("scalar", lambda e, r=r, h=h, hh=hh, ch=ch, bkb=bkb: e.activation(
                            out=Lt[r][:, h, :], in_=banks[bkb][:, h * 128:(h + 1) * 128], func=AF.Exp,
                            bias=negG[:, ch, hh:hh + 1]), reads=[bankb[bkb], db], writes=[Ltb[r]])
                    s.add("scalar", lambda e, r=r, bkb=bkb: e.activation(
                        out=Et[r][:].rearrange("p h l -> p (h l)"), in_=banks[bkb][:, :], func=AF.Exp),
                        reads=[bankb[bkb]], writes=[Etb[r]])
                    s.add("vector", lambda e, r=r, m=m, gj=gj, d=d: e.scalar_tensor_tensor(
                        out=Mt[m][:], in0=Lt[r][:], scalar=1.0, op0=ALU.min,
                        in1=cbm[gj][d][:].unsqueeze(1).to_broadcast([128, 4, 128]), op1=ALU.mult),
                        reads=[Ltb[r], cbmb[gj][d]], writes=[Mtb[m]])
                    s.add("gpsimd", lambda e, r=r, m=m, g=g, cs_=cs_: e.tensor_tensor(
                        out=Cp[m][:], in0=Et[r][:], in1=cT[:, g, cs_].unsqueeze(1).to_broadcast([128, 4, 128]),
                        op=ALU.mult), reads=[Etb[r], bcb], writes=[Cpb[m]])
                for h in range(4):
                    hl = g * 4 + h
                    blk = hl // 2
                    bk = ybanks[blk // 4]
                    col = (blk % 4) * 128
                    po = (hl % 2) * 64

                    def mm(e, j=j, ch=ch, h=h, hl=hl, bk=bk, col=col, po=po, mi=tuple(mi)):
                        out = banks[bk][po:po + 64, col:col + 128]
                        e.matmul(out, lhsT=xdt[j][0][:, hl * 64:(hl + 1) * 64], rhs=Mt[mi[0]][:, h, :],
                                 start=True, stop=False)
                        e.matmul(out, lhsT=prevs[:, 0, ch, hl * 64:(hl + 1) * 64], rhs=Cp[mi[0]][:, h, :],
                                 start=False, stop=False)
                        e.matmul(out, lhsT=xdt[j][1][:, hl * 64:(hl + 1) * 64], rhs=Mt[mi[1]][:, h, :],
                                 start=False, stop=False)
                        return e.matmul(out, lhsT=prevs[:, 1, ch, hl * 64:(hl + 1) * 64], rhs=Cp[mi[1]][:, h, :],
                                        start=False, stop=True)
                    s.add("tensor", mm, reads=[xdtb[j][0], xdtb[j][1], Mtb[mi[0]], Mtb[mi[1]], Cpb[mi[0]], Cpb[mi[1]],
                                               prevb[0][ch], prevb[1][ch]], writes=[bankb[bk]])
            s.add("scalar", lambda e, j=j: e.activation(out=za_s[j][:], in_=za_s[j][:], func=AF.Silu),
                  reads=[za_sb[j]], writes=[za_sb[j]])
            for blk in range(8):
                bk = ybanks[blk // 4]
                col = (blk % 4) * 128
                s.add("vector", lambda e, j=j, blk=blk, bk=bk, col=col: e.scalar_tensor_tensor(
                    out=yo[j][:, blk, :], in0=xs_s[j][:, blk, :], scalar=dcol[:, blk:blk + 1], op0=ALU.mult,
                    in1=banks[bk][:, col:col + 128], op1=ALU.add),
                    reads=[xs_sb[j], vb, bankb[bk]], writes=[yob[j]])
            s.add("gpsimd", lambda e, j=j: e.tensor_tensor(out=yo[j][:], in0=yo[j][:], in1=za_s[j][:], op=ALU.mult),
                  reads=[yob[j], za_sb[j]], writes=[yob[j]])
            s.add("sync", lambda e, j=j, cs_=cs_: e.dma_start(out=ygT[:, :, cs_].rearrange("i p t -> p i t"), in_=yo[j][:]),
                  reads=[yob[j]], dma=True)
        return c.finish()


def build_out0():
    from contextlib import ExitStack
    with ExitStack() as stack:
        G = Gemm(stack)
        c, s = G.c, G.s
        ygT = c.dram_in("ygT", [D, T])
        ucT = c.dram_in("ucT", [D, T])
        zbT = c.dram_in("zbT", [D, T])
        hT = c.dram_in("hT", [D, T])
        wl = c.dram_in("w", [KC, 128, 2 * D])
        h1T = c.dram_out("h1T", [D, T])
        sg, sgb = G.load_vec("ssd_g", KC)
        lg, lgb = G.load_vec("ln_g", KC)
        lb, lbb = G.load_vec("ln_b", KC)
        xs = c.sb([128, KC, T], F32, name="xs")
        xb = [Buf("x%d" % k) for k in range(KC)]
        act = c.sb([128, 2 * KC, T], BF16, name="act")
        actb = [Buf("a%d" % k) for k in range(2 * KC)]
        sq = c.sb([128, 2, T], BF16, name="sq")
        sqb = [Buf(), Buf()]
        xbf = c.sb([128, 2, T], BF16, name="xbf")
        xbfb = [Buf(), Buf()]
        rstd = c.sb([128, T], F32, name="rstd")
        rstdb = Buf("rstd")
        mean = c.sb([128, T], F32, name="mean")
        meanb = Buf("mean")
        msq = c.sb([128, T], F32, name="msq")
        msqb = Buf("msq")
        zb = c.sb([128, 2, T], F32, name="zb")
        zbb = [Buf(), Buf()]
        G.load_chunks(ygT, xs, xb, KC)
        G.rmsnorm_fm(xs, xb, sg, sgb, act, actb, 0, [0, 1, 2], sq, sqb, rstd, rstdb)
        G.load_chunks(ucT, xs, xb, KC)
        G.stats(xs, xb, KC, sq, sqb, [[3, 4, 5], [0, 1, 2]], want_mean=True, xbf=xbf, xbfb=xbfb)
        for tb in range(NTB):
            ts_ = slice(tb * TB, (tb + 1) * TB)
            s.add("scalar", lambda e, tb=tb, ts_=ts_: e.activation(out=mean[:, ts_], in_=G.banks[tb][:, :TB],
                                                                   func=AF.Identity, scale=1.0 / D),
                  reads=[G.bankbuf[tb]], writes=[meanb])
        s.add("vector", lambda e: e.tensor_tensor(out=msq[:], in0=mean[:], in1=mean[:], op=ALU.mult),
              reads=[meanb], writes=[msqb])
        for tb in range(NTB):
            ts_ = slice(tb * TB, (tb + 1) * TB)
            s.add("vector", lambda e, tb=tb, ts_=ts_: e.scalar_tensor_tensor(
                out=rstd[:, ts_], in0=G.banks[3 + tb][:, :TB], scalar=1.0 / D, op0=ALU.mult,
                in1=msq[:, ts_], op1=ALU.subtract), reads=[G.bankbuf[3 + tb], msqb], writes=[rstdb])
        s.add("scalar", lambda e: e.activation(out=rstd[:], in_=rstd[:], func=AF.Sqrt, bias=EPS),
              reads=[rstdb], writes=[rstdb])
        s.add("vector", lambda e: e.reciprocal(out=rstd[:], in_=rstd[:]), reads=[rstdb], writes=[rstdb])
        for k in range(KC):
            j = k % 2
            s.add("sync", lambda e, k=k, j=j: e.dma_start(out=zb[:, j, :], in_=zbT[k * 128:(k + 1) * 128, :]),
                  writes=[zbb[j]], dma=True)
            s.add("vector", lambda e, k=k: e.tensor_tensor(out=xs[:, k, :], in0=xs[:, k, :], in1=mean[:],
                                                           op=ALU.subtract), reads=[meanb, xb[k]], writes=[xb[k]])
            s.add("vector", lambda e, k=k: e.scalar_tensor_tensor(
                out=xs[:, k, :], in0=xs[:, k, :], scalar=lg[:, k:k + 1], op0=ALU.mult, in1=rstd[:], op1=ALU.mult),
                reads=[xb[k], lgb, rstdb], writes=[xb[k]])
            s.add("scalar", lambda e, k=k: e.activation(out=xs[:, k, :], in_=xs[:, k, :], func=AF.Silu,
                                                        bias=lb[:, k:k + 1]), reads=[xb[k], lbb], writes=[xb[k]])
            s.add("scalar", lambda e, j=j: e.activation(out=zb[:, j, :], in_=zb[:, j, :], func=AF.Silu),
                  reads=[zbb[j]], writes=[zbb[j]])
            s.add("gpsimd", lambda e, k=k, j=j: e.tensor_tensor(out=act[:, KC + k, :], in0=xs[:, k, :],
                                                                in1=zb[:, j, :], op=ALU.mult),
                  reads=[xb[k], zbb[j]], writes=[actb[KC + k]])
        G.load_chunks(hT, xs, xb, KC)
        stage = c.sb([128, 2, T], F32, name="stage")
        stageb = [Buf("st0"), Buf("st1")]

        def epi(nb, st):
            j = nb % 2
            for tb in range(NTB):
                ts_ = slice(tb * TB, (tb + 1) * TB)
                s.add("vector", lambda e, nb=nb, j=j, tb=tb, ts_=ts_, bk=st[tb]: e.tensor_tensor(
                    out=stage[:, j, ts_], in0=G.banks[bk][:, :TB], in1=xs[:, nb, ts_], op=ALU.add),
                    reads=[G.bankbuf[st[tb]], xb[nb]], writes=[stageb[j]])
            s.add("sync", lambda e, nb=nb, j=j: e.dma_start(
                out=h1T[nb * 128:(nb + 1) * 128, :], in_=stage[:, j, :]), reads=[stageb[j]], dma=True)

        G.main_loop(wl, KC, 2 * KC, act, actb, epi)
        return c.finish()


def build_out1():
    from contextlib import ExitStack
    with ExitStack() as stack:
        G = Gemm(stack)
        c, s = G.c, G.s
        ogT = c.dram_in("ogT", [D, T])
        hT = c.dram_in("hT", [D, T])
        wl = c.dram_in("w", [KC, 128, D])
        outT = c.dram_out("outT", [D, T])
        fg, fgb = G.load_vec("fg", KC)
        xs = c.sb([128, KC, T], F32, name="xs")
        xb = [Buf("x%d" % k) for k in range(KC)]
        act = c.sb([128, KC, T], BF16, name="act")
        actb = [Buf("a%d" % k) for k in range(KC)]
        sq = c.sb([128, 2, T], BF16, name="sq")
        sqb = [Buf(), Buf()]
        rstd = c.sb([128, T], F32, name="rstd")
        rstdb = Buf("rstd")
        for k in range(KC):
            s.add("gpsimd", lambda e, k=k: e.dma_start(out=act[:, k, :], in_=ogT[k * 128:(k + 1) * 128, :]),
                  writes=[actb[k]], dma=True)
        G.load_chunks(hT, xs, xb, KC)

        def epi(nb, st):
            for tb in range(NTB):
                ts_ = slice(tb * TB, (tb + 1) * TB)
                s.add("vector", lambda e, nb=nb, tb=tb, ts_=ts_, bk=st[tb]: e.tensor_tensor(
                    out=xs[:, nb, ts_], in0=G.banks[bk][:, :TB], in1=xs[:, nb, ts_], op=ALU.add),
                    reads=[G.bankbuf[st[tb]], xb[nb]], writes=[xb[nb]])

        G.main_loop(wl, KC, KC, act, actb, epi)
        G.stats(xs, xb, KC, sq, sqb, [[0, 1, 2]])
        G.rstd_from_ssq([0, 1, 2], rstd, rstdb, D)
        stage = c.sb([128, 2, T], F32, name="stage")
        stageb = [Buf("st0"), Buf("st1")]
        for k in range(KC):
            j = k % 2
            s.add("vector", lambda e, k=k, j=j: e.scalar_tensor_tensor(
                out=stage[:, j, :], in0=xs[:, k, :], scalar=fg[:, k:k + 1], op0=ALU.mult, in1=rstd[:], op1=ALU.mult),
                reads=[xb[k], fgb, rstdb], writes=[stageb[j]])
            s.add("sync", lambda e, k=k, j=j: e.dma_start(out=outT[k * 128:(k + 1) * 128, :], in_=stage[:, j, :]),
                  reads=[stageb[j]], dma=True)
        return c.finish()


GW = 64
GR = 32
NEG = -30000.0


def build_na(row_lo=-1, row_hi=GR):
    from contextlib import ExitStack
    with ExitStack() as stack:
        c = Ctx(stack)
        s = c.s
        nc = c.nc
        qT_d = c.dram_in("qT", [8, 128, L])
        kT_d = c.dram_in("kT", [8, 128, L])
        vtok = c.dram_in("vtok", [L, 1024])
        ztok = c.dram_in("ztok", [L, 1024])
        bm_d = c.dram_in("bmT", [16, 128, 14 * 64])
        og = c.dram_out("og", [L, 1024])
        banks = [c.ps([128, 512], F32, name="bank%d" % i) for i in range(8)]
        ident = c.sb([128, 128], BF16, name="ident")
        ib = Buf("ident")
        s.add("vector", lambda e: e.memset(ident[:], 1.0), writes=[ib])
        s.add("gpsimd", lambda e: e.affine_select(out=ident[:], in_=ident[:], pattern=[[-1, 128]], base=0,
                                                  channel_multiplier=1, compare_op=ALU.is_equal, fill=0.0),
              reads=[ib], writes=[ib])
        qT = c.sb([128, 8, L], BF16, name="qT_sb")
        kT = c.sb([128, 8, L], BF16, name="kT_sb")
        qkb = Buf("qk")
        for i in range(8):
            for (dst, src) in ((qT, qT_d), (kT, kT_d)):
                for hf in range(2):
                    s.add("gpsimd", lambda e, i=i, dst=dst, src=src, hf=hf: e.dma_start(
                        out=dst[:, i, hf * T:(hf + 1) * T], in_=src[i, :, hf * T:(hf + 1) * T]),
                        writes=[qkb], dma=True)
        Ve = c.sb([128, 16, 16, 65], BF16, name="Ve")
        Vo = c.sb([128, 15, 16, 65], BF16, name="Vo")
        Vm = c.sb([16, 16, 65], BF16, name="Vm")
        vb = Buf("v")
        s.add("vector", lambda e: e.memset(Ve[:], 1.0), writes=[vb])
        s.add("gpsimd", lambda e: e.memset(Vo[:], 1.0), writes=[vb])
        s.add("vector", lambda e: e.memset(Vm[:], 1.0), writes=[vb])
        vst = [c.sb([128, 1024], F32, name="vst%d" % i) for i in range(2)]
        vstb = [Buf(), Buf()]
        jobs = [(Ve, m, NMETA + 128 * m, 128) for m in range(16)] + [(Vo, m, NMETA + 64 + 128 * m, 128) for m in range(15)]
        for n, (dst, m, t0, npart) in enumerate(jobs):
            j = n % 2
            s.add("sync", lambda e, j=j, t0=t0: e.dma_start(out=vst[j][:], in_=vtok[t0:t0 + 128, :]),
                  writes=[vstb[j]], dma=True)
            eng = "vector" if n % 2 == 0 else "gpsimd"
            s.add(eng, lambda e, j=j, dst=dst, m=m: e.tensor_copy(
                out=dst[:, m, :, 0:64], in_=vst[j][:].rearrange("p (h d) -> p h d", d=64)),
                reads=[vstb[j]], writes=[vb])
        s.add("sync", lambda e: e.dma_start(out=vst[0][0:NMETA, :], in_=vtok[0:NMETA, :]), writes=[vstb[0]], dma=True)
        s.add("vector", lambda e: e.tensor_copy(out=Vm[:, :, 0:64],
                                                in_=vst[0][0:NMETA, :].rearrange("p (h d) -> p h d", d=64)),
              reads=[vstb[0]], writes=[vb])
        bm = c.sb([128, 16, 14 * 64], BF16, name="bm_sb")
        bmb = Buf("bm")
        for h in range(16):
            s.add("gpsimd", lambda e, h=h: e.dma_start(out=bm[:, h, :], in_=bm_d[h]), writes=[bmb], dma=True)
        for h4 in range(4):
            s.add("vector", lambda e, h4=h4: e.tensor_scalar(out=bm[:, h4 * 4:(h4 + 1) * 4, :], in0=bm[:, h4 * 4:(h4 + 1) * 4, :],
                                                             scalar1=8.0, scalar2=None, op0=ALU.mult),
                  reads=[bmb], writes=[bmb])

        NP = 3
        PT = [c.sb([128, 512], BF16, name="PT%d" % i) for i in range(NP)]
        PTb = [Buf() for _ in range(NP)]
        PmT = [c.sb([16, 128], BF16, name="PmT%d" % i) for i in range(NP)]
        PmTb = [Buf() for _ in range(NP)]
        rec = [c.sb([64, 2], F32, name="rec%d" % i) for i in range(NP)]
        recb = [Buf() for _ in range(NP)]
        zs = [c.sb([64, 1024], F32, name="zs%d" % i) for i in range(2)]
        zsb = [Buf(), Buf()]
        orow = [c.sb([64, 1024], F32, name="orow%d" % i) for i in range(2)]
        orowb = [Buf(), Buf()]
        Ab = [Buf(), Buf()]
        Mb = [Buf(), Buf()]
        Ob = [Buf(), Buf()]
        ip = 0
        for ri in range(row_lo, row_hi):
            jr = (ri + 1) % 2
            if ri < 0:
                nq, q0 = NMETA, 0
            else:
                nq, q0 = GW, NMETA + GW * ri
                rs = min(max(ri - 4, 0), GR - 8)
                o = rs - ri
            s.add("sync", lambda e, jr=jr, nq=nq, q0=q0: e.dma_start(out=zs[jr][0:nq, :], in_=ztok[q0:q0 + nq, :]),
                  writes=[zsb[jr]], dma=True)
            s.add("scalar", lambda e, jr=jr, nq=nq: e.activation(out=zs[jr][0:nq, :], in_=zs[jr][0:nq, :], func=AF.Silu),
                  reads=[zsb[jr]], writes=[zsb[jr]])
            for i in range(8):
                p = ip % NP
                ja = ip % 2
                ip += 1

                def mm1(e, i=i, ja=ja, ri=ri, nq=nq, q0=q0):
                    inst = None
                    for hs in range(2):
                        hp = slice(hs * 64, hs * 64 + 64)
                        inst = e.matmul(banks[4 + hs][0:16, ja * 64: ja * 64 + nq],
                                        lhsT=kT[hp, i, 0:NMETA], rhs=qT[hp, i, q0:q0 + nq], start=True, stop=True)
                    if ri < 0:
                        return inst
                    rs = min(max(ri - 4, 0), GR - 8)
                    o = rs - ri
                    for hs in range(2):
                        hp = slice(hs * 64, hs * 64 + 64)
                        hl = 2 * i + hs
                        e.matmul(banks[ja * 2 + hs][:, 0:256].rearrange("p (k q) -> p k q", q=64),
                                 lhsT=ident[:],
                                 rhs=bm[:, hl, :].rearrange("p (d q) -> p d q", q=64)[:, o + 7:o + 14:2, :],
                                 start=True, stop=False)
                        for kb in range(4):
                            k0 = NMETA + GW * (rs + 2 * kb)
                            inst = e.matmul(banks[ja * 2 + hs][:, kb * 64:(kb + 1) * 64],
                                            lhsT=kT[hp, i, k0:k0 + 128], rhs=qT[hp, i, q0:q0 + GW],
                                            start=False, stop=(kb == 3))
                    return inst
                s.add("tensor", mm1, reads=[qkb, ib, bmb], writes=[Ab[ja], Mb[ja]])
                for hs in range(2):
                    if ri >= 0:
                        s.add("scalar", lambda e, p=p, ja=ja, hs=hs: e.activation(
                            out=PT[p][:, hs * 256:(hs + 1) * 256], in_=banks[ja * 2 + hs][:, 0:256], func=AF.Exp,
                            scale=0.125), reads=[Ab[ja]], writes=[PTb[p]])
                    s.add("scalar", lambda e, p=p, ja=ja, hs=hs, nq=nq: e.activation(
                        out=PmT[p][:, hs * 64: hs * 64 + nq], in_=banks[4 + hs][0:16, ja * 64: ja * 64 + nq],
                        func=AF.Exp, scale=0.125), reads=[Mb[ja]], writes=[PmTb[p]])

                def mm2(e, i=i, ja=ja, p=p, ri=ri, nq=nq):
                    inst = None
                    for hs in range(2):
                        hl = 2 * i + hs
                        out = banks[6][0:nq, ja * 256 + hs * 65: ja * 256 + hs * 65 + 65]
                        if ri >= 0:
                            rs = min(max(ri - 4, 0), GR - 8)
                            for kb in range(4):
                                rr = rs + 2 * kb
                                vv = Ve[:, rr // 2, hl, :] if rr % 2 == 0 else Vo[:, (rr - 1) // 2, hl, :]
                                e.matmul(out, lhsT=PT[p][:, hs * 256 + kb * 64: hs * 256 + (kb + 1) * 64], rhs=vv,
                                         start=(kb == 0), stop=False)
                        inst = e.matmul(out, lhsT=PmT[p][0:16, hs * 64: hs * 64 + nq], rhs=Vm[:, hl, :],
                                        start=(ri < 0), stop=True)
                    return inst
                s.add("tensor", mm2, reads=[PTb[p], PmTb[p], vb], writes=[Ob[ja]])
                s.add("vector", lambda e, p=p, ja=ja, nq=nq: e.reciprocal(
                    out=rec[p][0:nq, :], in_=banks[6][0:nq, ja * 256: ja * 256 + 130].rearrange("p (h d) -> p h d", d=65)[:, :, 64]),
                    reads=[Ob[ja]], writes=[recb[p]])
                s.add("vector", lambda e, p=p, ja=ja, nq=nq, i=i, jr=jr: e.tensor_scalar(
                    out=orow[jr][0:nq, (2 * i) * 64:(2 * i + 1) * 64], in0=banks[6][0:nq, ja * 256: ja * 256 + 64],
                    scalar1=rec[p][0:nq, 0:1], scalar2=None, op0=ALU.mult),
                    reads=[Ob[ja], recb[p]], writes=[orowb[jr]])
                s.add("scalar", lambda e, p=p, ja=ja, nq=nq, i=i, jr=jr: e.activation(
                    out=orow[jr][0:nq, (2 * i + 1) * 64:(2 * i + 2) * 64], in_=banks[6][0:nq, ja * 256 + 65: ja * 256 + 129],
                    func=AF.Identity, scale=rec[p][0:nq, 1:2]),
                    reads=[Ob[ja], recb[p]], writes=[orowb[jr]])
            s.add("gpsimd", lambda e, jr=jr, nq=nq: e.tensor_tensor(out=orow[jr][0:nq, :], in0=orow[jr][0:nq, :],
                                                                    in1=zs[jr][0:nq, :], op=ALU.mult),
                  reads=[orowb[jr], zsb[jr]], writes=[orowb[jr]])
            s.add("sync", lambda e, jr=jr, nq=nq, q0=q0: e.dma_start(out=og[q0:q0 + nq, :], in_=orow[jr][0:nq, :]),
                  reads=[orowb[jr]], dma=True)
        return c.finish()


def na_bias_layout(rpb_h):
    nh = rpb_h.shape[0]
    col = np.arange(GW)
    cs0 = np.clip(col - 8, 0, GW - 16)
    cmask = (col[None, :] >= cs0[:, None]) & (col[None, :] < cs0[:, None] + 16)
    cidx = np.clip(col[None, :] - col[:, None] + 15, 0, 30)
    out = np.empty((nh, 2, GW, 14, GW), np.float32)
    for jp in range(2):
        for dd in range(14):
            g = rpb_h[:, dd + jp, :][:, cidx]
            g = np.where(cmask[None], g, np.float32(NEG))
            out[:, jp, :, dd, :] = g.transpose(0, 2, 1)
    return np.ascontiguousarray(out.reshape(nh, 128, 14 * GW))


def _f32(a):
    return np.ascontiguousarray(a, dtype=np.float32)


def _wl_layout(W, nblocks):
    K = W.shape[0]
    return _f32(W.reshape(K // 128, 128, nblocks, 128).transpose(2, 1, 0, 3).reshape(nblocks, 128, K))


def _vec(g):
    return _f32(g.reshape(-1, 128).T)


def _padcols(a):
    return np.concatenate([np.zeros((a.shape[0], LP - L), np.float32), a], axis=1)


def _run(nc, in_maps):
    res = run_bass_kernel_spmd(nc, in_maps, core_ids=list(range(N_CORES)))
    return res.results


def kernel(x, meta_tokens, e_norm_g, e_w_in, e_conv_w, e_conv_b, e_dt_bias, e_A_log, e_D,
           e_ssd_norm_g, e_dw_w, e_dw_b, e_ln_g, e_ln_b, e_w_out,
           o_norm_g, o_w_in, o_rpb, o_w_out, final_norm_g):
    x = np.asarray(x, np.float32)
    nb = x.shape[0]
    h0 = np.concatenate([np.broadcast_to(np.asarray(meta_tokens, np.float32)[None], (nb, NMETA, D)), x], axis=1)
    h0T = [[_f32(h0[b, hf * T:(hf + 1) * T, :].T) for hf in range(2)] for b in range(nb)]
    cores = [(c // 2, c % 2) for c in range(N_CORES)]

    Wi = np.asarray(e_w_in[0], np.float32)
    perm = np.concatenate([np.arange(0, 6144), np.arange(6208, 12352), np.arange(6144, 6208)])
    Wp = np.concatenate([Wi[:, perm], np.zeros((D, IN_E_PAD - IN_E), np.float32)], axis=1)
    wl = _wl_layout(Wp, 97)
    g0 = _vec(np.asarray(e_norm_g[0], np.float32))
    r = _run(build_in_proj(97), [{"hT": h0T[b][hf], "w": wl, "g": g0} for (b, hf) in cores])
    projT = [np.concatenate([r[2 * b]["outT"], r[2 * b + 1]["outT"]], axis=1) for b in range(nb)]
    del r

    cw = np.asarray(e_conv_w[0], np.float32)
    cbv = np.asarray(e_conv_b[0], np.float32)
    dw = np.asarray(e_dw_w[0], np.float32)
    dbv = np.asarray(e_dw_b[0], np.float32)
    maps = []
    for (b, jj) in cores:
        q5 = range(16 * jj, 16 * jj + 16)
        q31 = range(8 * jj, 8 * jj + 8)
        maps.append({
            "xin": _f32(np.stack([projT[b][2048 + 128 * q: 2048 + 128 * (q + 1)] for q in q5])),
            "cw5": _f32(np.stack([cw[:, 128 * q:128 * (q + 1)].T for q in q5], axis=1)),
            "cb5": _f32(np.stack([cbv[128 * q:128 * (q + 1)] for q in q5], axis=1)),
            "gv": _f32(np.stack([projT[b][8192 + 128 * q: 8192 + 128 * (q + 1)] for q in q31])),
            "gg": _f32(np.stack([projT[b][10240 + 128 * q: 10240 + 128 * (q + 1)] for q in q31])),
            "cw31": _f32(np.stack([dw[:, 128 * q:128 * (q + 1)].T for q in q31], axis=1)),
            "cb31": _f32(np.stack([dbv[128 * q:128 * (q + 1)] for q in q31], axis=1)),
        })
    r = _run(build_conv(16, 8), maps)
    xbcT = [np.concatenate([r[2 * b]["xout"].reshape(2048, L), r[2 * b + 1]["xout"].reshape(2048, L)], axis=0)
            for b in range(nb)]
    ucT = [np.concatenate([r[2 * b]["cout"].reshape(1024, L), r[2 * b + 1]["cout"].reshape(1024, L)], axis=0)
           for b in range(nb)]
    del r

    dbias = np.asarray(e_dt_bias[0], np.float32)
    alog = np.asarray(e_A_log[0], np.float32)
    Dv = np.asarray(e_D[0], np.float32)
    maps = []
    for (b, jh) in cores:
        hs = slice(16 * jh, 16 * jh + 16)
        xs_p = _padcols(xbcT[b][1024 * jh:1024 * (jh + 1)])
        B_p = _padcols(xbcT[b][2048 + 512 * jh: 2048 + 512 * (jh + 1)])
        C_p = _padcols(xbcT[b][3072 + 512 * jh: 3072 + 512 * (jh + 1)])
        dt_p = _padcols(np.concatenate([projT[b][12288 + 16 * jh: 12288 + 16 * (jh + 1)],
                                        projT[b][12320 + 16 * jh: 12320 + 16 * (jh + 1)]], axis=0))
        za_p = _padcols(projT[b][1024 * jh:1024 * (jh + 1)])
        maps.append({
            "xtok": _f32(xs_p.T.reshape(NCH, 128, 1024)),
            "btok": _f32(B_p.T.reshape(NCH, 128, 512)),
            "bT": _f32(B_p.reshape(GL, 128, LP)),
            "cT": _f32(C_p.reshape(GL, 128, LP)),
            "dtraw": _f32(dt_p.T.reshape(NCH, 128, 32)),
            "dtbias": _f32(np.broadcast_to(np.concatenate([dbias[0, hs], dbias[1, hs]])[None], (128, 32))),
            "alog": _f32(np.broadcast_to(np.concatenate([alog[0, hs], alog[1, hs]])[None], (128, 32))),
            "dcol": _f32(np.repeat(Dv[hs], 64).reshape(8, 128).T),
            "xsT": _f32(xs_p.reshape(8, 128, LP)),
            "zaT": _f32(za_p.reshape(8, 128, LP)),
        })
    r = _run(build_ssd(), maps)
    ygT = [np.concatenate([r[2 * b]["ygT"].reshape(1024, LP)[:, LP - L:], r[2 * b + 1]["ygT"].reshape(1024, LP)[:, LP - L:]],
                          axis=0) for b in range(nb)]
    del r, xbcT

    wl = _wl_layout(np.asarray(e_w_out[0], np.float32), 16)
    sg, lg, lb = _vec(np.asarray(e_ssd_norm_g[0])), _vec(np.asarray(e_ln_g[0])), _vec(np.asarray(e_ln_b[0]))
    maps = []
    for (b, hf) in cores:
        ts_ = slice(hf * T, (hf + 1) * T)
        maps.append({"ygT": _f32(ygT[b][:, ts_]), "ucT": _f32(ucT[b][:, ts_]), "zbT": _f32(projT[b][6144:8192, ts_]),
                     "hT": h0T[b][hf], "w": wl, "ssd_g": sg, "ln_g": lg, "ln_b": lb})
    r = _run(build_out0(), maps)
    h1T = [[r[2 * b + hf]["h1T"] for hf in range(2)] for b in range(nb)]
    del r, projT, ygT, ucT

    wl = _wl_layout(np.asarray(o_w_in[0], np.float32), 64)
    g1 = _vec(np.asarray(o_norm_g[0], np.float32))
    r = _run(build_in_proj(64), [{"hT": h1T[b][hf], "w": wl, "g": g1} for (b, hf) in cores])
    p1T = [np.concatenate([r[2 * b]["outT"], r[2 * b + 1]["outT"]], axis=1) for b in range(nb)]
    del r

    rpb = np.asarray(o_rpb[0], np.float32)
    maps = []
    for (b, jh) in cores:
        cs_ = slice(1024 * jh, 1024 * (jh + 1))
        maps.append({"qT": _f32(p1T[b][0:2048][cs_].reshape(8, 128, L)),
                     "kT": _f32(p1T[b][2048:4096][cs_].reshape(8, 128, L)),
                     "vtok": _f32(p1T[b][4096:6144][cs_].T),
                     "ztok": _f32(p1T[b][6144:8192][cs_].T),
                     "bmT": na_bias_layout(rpb[16 * jh:16 * (jh + 1)])})
    r = _run(build_na(), maps)
    ogT = [np.concatenate([r[2 * b]["og"].T, r[2 * b + 1]["og"].T], axis=0) for b in range(nb)]
    del r, p1T

    wl = _wl_layout(np.asarray(o_w_out[0], np.float32), 16)
    fg = _vec(np.asarray(final_norm_g, np.float32))
    maps = []
    for (b, hf) in cores:
        ts_ = slice(hf * T, (hf + 1) * T)
        maps.append({"ogT": _f32(ogT[b][:, ts_]), "hT": h1T[b][hf], "w": wl, "fg": fg})
    r = _run(build_out1(), maps)
    out = np.stack([np.concatenate([r[2 * b]["outT"], r[2 * b + 1]["outT"]], axis=1).T[NMETA:] for b in range(nb)])
    return _f32(out)
```
